# Optimizing a Trainium2 kernel written in Bass

```python
import math
import jax, jax.numpy as jnp
from jax import lax
import numpy as np

D_MODEL = 1024
BATCH = 1
SEQ = 16384
DEPTH = 4

EPS = 1e-6
BLOCK = 128

MLA_HEADS = 8
MLA_Q_LORA = 256
MLA_KV_LORA = 128
MLA_NOPE = 64
MLA_ROPE = 32
MLA_V = 64
ROPE_BASE = 10000.0

DIL_HEADS = 8
DIL_HEAD_DIM = 64
DIL_PATTERNS = ((128, 1), (512, 4), (2048, 16))

SWA_Q_HEADS = 16
SWA_KV_HEADS = 2
SWA_HEAD_DIM = 64
SWA_WINDOW = 128

D_FF = -(-8 * D_MODEL // (3 * 256)) * 256

N_EVEN = (DEPTH + 1) // 2
N_ODD = DEPTH // 2

MIX_IN_A = MLA_Q_LORA + MLA_KV_LORA + MLA_ROPE
MIX_IN_B = 3 * DIL_HEADS * DIL_HEAD_DIM
MIX_IN = MIX_IN_A + MIX_IN_B
MIX_OUT = MLA_HEADS * MLA_V + DIL_HEADS * DIL_HEAD_DIM
MLA_QK = MLA_NOPE + MLA_ROPE
SWA_QKV = (SWA_Q_HEADS + 2 * SWA_KV_HEADS) * SWA_HEAD_DIM
SWA_OUT = SWA_Q_HEADS * SWA_HEAD_DIM

kernel_name = "hybrid_mla_dilated_swa_sink_trunk"


def _rmsnorm(x, g):
    xf = x.astype(jnp.float32)
    y = xf * lax.rsqrt(jnp.mean(xf * xf, axis=-1, keepdims=True) + EPS)
    return (y * g.astype(jnp.float32)).astype(x.dtype)


def _alibi_slopes(n):
    return jnp.exp2(-8.0 * jnp.arange(1, n + 1, dtype=jnp.float32) / n)


def _rope_tables(seq, dim):
    pos = jnp.arange(seq, dtype=jnp.float32)
    inv_freq = ROPE_BASE ** (-jnp.arange(0, dim, 2, dtype=jnp.float32) / dim)
    ang = pos[:, None] * inv_freq[None, :]
    return jnp.cos(ang), jnp.sin(ang)


def _rope(x, cos, sin):
    half = x.shape[-1] // 2
    x1, x2 = x[..., :half], x[..., half:]
    c = cos.astype(x.dtype)
    s = sin.astype(x.dtype)
    return jnp.concatenate([x1 * c - x2 * s, x2 * c + x1 * s], axis=-1)


def _to_blocks(t):
    b, s, h, d = t.shape
    return t.reshape(b, s // BLOCK, BLOCK, h, d).transpose(1, 0, 2, 3, 4)


def _from_blocks(t):
    nb, b, blk, h, d = t.shape
    return t.transpose(1, 0, 2, 3, 4).reshape(b, nb * blk, h * d)


def _mla_causal_attention(q, k, v):
    S = q.shape[1]
    nb = S // BLOCK
    scale = 1.0 / math.sqrt(q.shape[-1])
    kpos = jnp.arange(S)

    def one(args):
        qi, bi = args
        s = jnp.einsum('bqhd,bshd->bhqs', qi, k).astype(jnp.float32) * scale
        qpos = bi * BLOCK + jnp.arange(BLOCK)
        s = jnp.where((kpos[None, :] <= qpos[:, None])[None, None], s, -jnp.inf)
        p = jax.nn.softmax(s, axis=-1)
        return jnp.einsum('bhqs,bshd->bqhd', p.astype(v.dtype), v)

    out = lax.map(one, (_to_blocks(q), jnp.arange(nb)))
    return _from_blocks(out)


def _dilated_attention(q, k, v, slopes):
    S = q.shape[1]
    nb = S // BLOCK
    scale = 1.0 / math.sqrt(q.shape[-1])

    def one(args):
        qi, bi = args
        qpos = bi * BLOCK + jnp.arange(BLOCK)
        outs, lses = [], []
        for window, dil in DIL_PATTERNS:
            offs = dil * jnp.arange(window // dil + 1)
            kpos = qpos[:, None] - offs[None, :]
            valid = kpos >= 0
            idx = jnp.maximum(kpos, 0)
            kg = jnp.take(k, idx, axis=1)
            vg = jnp.take(v, idx, axis=1)
            s = jnp.einsum('bqhd,bqjhd->bhqj', qi, kg).astype(jnp.float32) * scale
            s = s - slopes[:, None, None] * offs.astype(jnp.float32)[None, None, :]
            s = jnp.where(valid[None, None], s, -jnp.inf)
            lse = jax.nn.logsumexp(s, axis=-1)
            p = jnp.exp(s - lse[..., None])
            outs.append(jnp.einsum('bhqj,bqjhd->bqhd', p.astype(v.dtype), vg))
            lses.append(lse)
        w = jax.nn.softmax(jnp.stack(lses, axis=0), axis=0)
        w = w.transpose(0, 1, 3, 2)[..., None].astype(v.dtype)
        return jnp.sum(w * jnp.stack(outs, axis=0), axis=0)

    out = lax.map(one, (_to_blocks(q), jnp.arange(nb)))
    return _from_blocks(out)


def _swa_sink_attention(q, k, v, sinks, slopes):
    B_, S, Hq, dh = q.shape
    Hkv = k.shape[2]
    G = Hq // Hkv
    nb = S // BLOCK
    scale = 1.0 / math.sqrt(dh)
    qb = q.reshape(B_, nb, BLOCK, Hkv, G, dh)

    def band(t):
        cur = t.reshape(B_, nb, BLOCK, Hkv, dh)
        prev = jnp.pad(t, ((0, 0), (BLOCK, 0), (0, 0), (0, 0)))[:, :S].reshape(B_, nb, BLOCK, Hkv, dh)
        return jnp.concatenate([prev, cur], axis=2)

    kb, vb = band(k), band(v)
    qi = jnp.arange(BLOCK)
    kj = jnp.arange(2 * BLOCK)
    dist = qi[:, None] + BLOCK - kj[None, :]
    kpos = jnp.arange(nb)[:, None] * BLOCK - BLOCK + kj[None, :]
    valid = ((dist >= 0) & (dist < SWA_WINDOW))[None] & (kpos >= 0)[:, None, :]
    s = jnp.einsum('bnqkgd,bnjkd->bnkgqj', qb, kb).astype(jnp.float32) * scale
    s = s - slopes.reshape(Hkv, G)[:, :, None, None] * dist.astype(jnp.float32)
    s = jnp.where(valid[None, :, None, None], s, -jnp.inf)
    sink = jnp.broadcast_to(sinks.astype(jnp.float32).reshape(1, 1, Hkv, G, 1, 1), s.shape[:-1] + (1,))
    p = jax.nn.softmax(jnp.concatenate([s, sink], axis=-1), axis=-1)[..., :-1]
    o = jnp.einsum('bnkgqj,bnjkd->bnqkgd', p.astype(v.dtype), vb)
    return o.reshape(B_, S, Hq * dh)


def _even_mixer(h, w_in, g_q, g_kv, w_uq, w_ukv, w_out, cos, sin, dil_slopes):
    B_, S, _ = h.shape
    proj = h @ w_in
    o1 = MLA_Q_LORA
    o2 = o1 + MLA_KV_LORA
    o3 = o2 + MLA_ROPE
    nd = DIL_HEADS * DIL_HEAD_DIM
    c_q, c_kv, k_r = proj[..., :o1], proj[..., o1:o2], proj[..., o2:o3]
    qB = proj[..., o3:o3 + nd].reshape(B_, S, DIL_HEADS, DIL_HEAD_DIM)
    kB = proj[..., o3 + nd:o3 + 2 * nd].reshape(B_, S, DIL_HEADS, DIL_HEAD_DIM)
    vB = proj[..., o3 + 2 * nd:o3 + 3 * nd].reshape(B_, S, DIL_HEADS, DIL_HEAD_DIM)
    qA = (_rmsnorm(c_q, g_q) @ w_uq).reshape(B_, S, MLA_HEADS, MLA_QK)
    q_nope, q_rope = qA[..., :MLA_NOPE], qA[..., MLA_NOPE:]
    q_rope = _rope(q_rope, cos[None, :, None, :], sin[None, :, None, :])
    kv = (_rmsnorm(c_kv, g_kv) @ w_ukv).reshape(B_, S, MLA_HEADS, MLA_NOPE + MLA_V)
    k_nope, vA = kv[..., :MLA_NOPE], kv[..., MLA_NOPE:]
    k_rope = _rope(k_r, cos[None], sin[None])
    k_rope = jnp.broadcast_to(k_rope[:, :, None, :], (B_, S, MLA_HEADS, MLA_ROPE))
    oA = _mla_causal_attention(jnp.concatenate([q_nope, q_rope], axis=-1),
                               jnp.concatenate([k_nope, k_rope], axis=-1), vA)
    oB = _dilated_attention(qB, kB, vB, dil_slopes)
    return jnp.concatenate([oA, oB], axis=-1) @ w_out


def _odd_mixer(h, w_qkv, sinks, w_out, swa_slopes):
    B_, S, _ = h.shape
    qkv = h @ w_qkv
    nq = SWA_Q_HEADS * SWA_HEAD_DIM
    nk = SWA_KV_HEADS * SWA_HEAD_DIM
    q = qkv[..., :nq].reshape(B_, S, SWA_Q_HEADS, SWA_HEAD_DIM)
    k = qkv[..., nq:nq + nk].reshape(B_, S, SWA_KV_HEADS, SWA_HEAD_DIM)
    v = qkv[..., nq + nk:].reshape(B_, S, SWA_KV_HEADS, SWA_HEAD_DIM)
    return _swa_sink_attention(q, k, v, sinks, swa_slopes) @ w_out


def _swiglu(h, w_gate_up, w_down):
    gu = h @ w_gate_up
    gate, up = gu[..., :D_FF], gu[..., D_FF:]
    return (jax.nn.silu(gate) * up) @ w_down


def setup_inputs(seed: int = 0) -> dict:
    key = jax.random.key(seed)
    ks = jax.random.split(key, 16)
    f32 = jnp.float32

    def w(k, shape, fan_in):
        return jax.random.normal(k, shape, f32) * (fan_in ** -0.5)

    def gain(k, shape):
        return 1.0 + 0.05 * jax.random.normal(k, shape, f32)

    return {
        "x": jax.random.normal(ks[0], (BATCH, SEQ, D_MODEL), f32),
        "attn_norm": gain(ks[1], (DEPTH, D_MODEL)),
        "ffn_norm": gain(ks[2], (DEPTH, D_MODEL)),
        "final_norm": gain(ks[3], (D_MODEL,)),
        "e_w_in": w(ks[4], (N_EVEN, D_MODEL, MIX_IN), D_MODEL),
        "e_q_norm": gain(ks[5], (N_EVEN, MLA_Q_LORA)),
        "e_kv_norm": gain(ks[6], (N_EVEN, MLA_KV_LORA)),
        "e_w_uq": w(ks[7], (N_EVEN, MLA_Q_LORA, MLA_HEADS * MLA_QK), MLA_Q_LORA),
        "e_w_ukv": w(ks[8], (N_EVEN, MLA_KV_LORA, MLA_HEADS * (MLA_NOPE + MLA_V)), MLA_KV_LORA),
        "e_w_out": w(ks[9], (N_EVEN, MIX_OUT, D_MODEL), MIX_OUT),
        "o_w_qkv": w(ks[10], (N_ODD, D_MODEL, SWA_QKV), D_MODEL),
        "o_sinks": jax.random.normal(ks[11], (N_ODD, SWA_Q_HEADS), f32),
        "o_w_out": w(ks[12], (N_ODD, SWA_OUT, D_MODEL), SWA_OUT),
        "f_w_gate_up": w(ks[13], (DEPTH, D_MODEL, 2 * D_FF), D_MODEL),
        "f_w_down": w(ks[14], (DEPTH, D_FF, D_MODEL), D_FF),
    }


def reference(x, attn_norm, ffn_norm, final_norm, e_w_in, e_q_norm, e_kv_norm, e_w_uq,
              e_w_ukv, e_w_out, o_w_qkv, o_sinks, o_w_out, f_w_gate_up, f_w_down):
    S = x.shape[1]
    cos, sin = _rope_tables(S, MLA_ROPE)
    dil_slopes = _alibi_slopes(DIL_HEADS)
    swa_slopes = _alibi_slopes(SWA_Q_HEADS)
    h = x
    for layer in range(DEPTH):
        i = layer // 2
        hn = _rmsnorm(h, attn_norm[layer])
        if layer % 2 == 0:
            mix = _even_mixer(hn, e_w_in[i], e_q_norm[i], e_kv_norm[i], e_w_uq[i],
                              e_w_ukv[i], e_w_out[i], cos, sin, dil_slopes)
        else:
            mix = _odd_mixer(hn, o_w_qkv[i], o_sinks[i], o_w_out[i], swa_slopes)
        h = h + mix
        h = h + _swiglu(_rmsnorm(h, ffn_norm[layer]), f_w_gate_up[layer], f_w_down[layer])
    return _rmsnorm(h, final_norm)
```

```python
from contextlib import ExitStack
import math
import numpy as np
import ml_dtypes
import concourse.bass as bass
import concourse.mybir as mybir
from concourse.bass_utils import run_bass_kernel_spmd

F32 = mybir.dt.float32
BF16 = mybir.dt.bfloat16
AF = mybir.ActivationFunctionType
ALU = mybir.AluOpType
NPBF = ml_dtypes.bfloat16

NCORES = 8
S = 16384
D = 1024
TPC = S // NCORES
DFF = 2816
EPS = 1e-6
FI_E = 1984
FI_O = 1280
NEG = -30000.0


class Op:
    __slots__ = ("eng", "fn", "deps", "inc", "dma", "sem", "val", "incv")


class Prog:
    ENGS = ("pe", "act", "dve", "pool", "sp")

    def __init__(self, nc):
        self.nc = nc
        self.ops = {e: [] for e in self.ENGS}
        self.lastw = {}
        self.readers = {}
        self.pcount = []
        self.keymap = {}
        self.free = []
        self.lastdma = {}
        self.lastop = {}
        self.pending = {e: [] for e in self.ENGS}

    def add(self, eng, fn, reads=(), writes=(), dma=False, semkey=None, incv=16):
        op = Op()
        op.eng, op.fn, op.dma, op.inc, op.sem, op.val = eng, fn, dma, False, None, 0
        deps = list(self.pending[eng])
        self.pending[eng] = []
        for k in reads:
            w = self.lastw.get(k)
            if w is not None:
                deps.append(w)
        for k in writes:
            w = self.lastw.get(k)
            if w is not None:
                deps.append(w)
            rd = self.readers.get(k)
            if rd:
                deps.extend(rd.values())
        seen = set()
        out = []
        for d in deps:
            if id(d) in seen:
                continue
            seen.add(id(d))
            if (not d.dma) and (not dma) and d.eng == "pe" and eng == "pe":
                continue
            out.append(d)
            if not d.dma:
                d.inc = True
        op.deps = out
        if dma:
            if semkey is None:
                semkey = writes[0] if writes else reads[0]
            ph = self.keymap.get(semkey)
            if ph is None:
                if self.free:
                    ph = self.free.pop()
                else:
                    ph = len(self.pcount)
                    self.pcount.append(0)
                self.keymap[semkey] = ph
            self.pcount[ph] += incv
            op.sem, op.val = ph, self.pcount[ph]
            op.incv = incv
            self.lastdma[ph] = op
        else:
            self.lastop[eng] = op
        self.ops[eng].append(op)
        for k in writes:
            self.lastw[k] = op
            self.readers[k] = {}
        for k in reads:
            slot = ("d", id(op)) if dma else eng
            self.readers.setdefault(k, {})[slot] = op
        return op

    def barrier(self):
        toks = []
        for e in self.ENGS:
            o = self.lastop.get(e)
            if o is not None:
                o.inc = True
                toks.append(o)
        toks.extend(self.lastdma.values())
        for e in self.ENGS:
            self.pending[e] = list(toks)
        self.lastw.clear()
        self.readers.clear()
        self.keymap.clear()
        self.free = list(range(len(self.pcount)))

    def dma(self, out, in_, reads=(), writes=(), q="sp", semkey=None):
        return self.add(q, lambda e: e.dma_start(out=out, in_=in_), reads, writes, dma=True, semkey=semkey)

    def mm(self, out, lhsT, rhs, start=True, stop=True, reads=(), writes=()):
        return self.add("pe", lambda e: e.matmul(out, lhsT, rhs, start=start, stop=stop), reads, writes)

    def tr(self, out, in_, ident, reads=(), writes=()):
        return self.add("pe", lambda e: e.transpose(out, in_, ident), reads, writes)

    def actf(self, out, in_, func, reads=(), writes=(), **kw):
        return self.add("act", lambda e: e.activation(out, in_, func, **kw), reads, writes)

    def ts(self, eng, out, in0, s1, s2, op0, op1, reads=(), writes=()):
        return self.add(eng, lambda e: e.tensor_scalar(out, in0, s1, s2, op0, op1), reads, writes)

    def tsmul(self, eng, out, in0, s1, reads=(), writes=()):
        return self.add(eng, lambda e: e.tensor_scalar_mul(out, in0, s1), reads, writes)

    def tsadd(self, eng, out, in0, s1, reads=(), writes=()):
        return self.add(eng, lambda e: e.tensor_scalar_add(out, in0, s1), reads, writes)

    def tt(self, eng, out, in0, in1, op, reads=(), writes=()):
        return self.add(eng, lambda e: e.tensor_tensor(out, in0, in1, op), reads, writes)

    def stt(self, eng, out, in0, scalar, in1, op0, op1, reads=(), writes=()):
        return self.add(eng, lambda e: e.scalar_tensor_tensor(out, in0, scalar, in1, op0, op1), reads, writes)

    def recip(self, out, in_, reads=(), writes=()):
        return self.add("dve", lambda e: e.reciprocal(out, in_), reads, writes)

    def cp(self, eng, out, in_, reads=(), writes=()):
        if eng == "act":
            return self.add(eng, lambda e: e.copy(out, in_), reads, writes)
        return self.add(eng, lambda e: e.tensor_copy(out, in_), reads, writes)

    def memset(self, eng, ap, val, writes=()):
        return self.add(eng, lambda e: e.memset(ap, val), (), writes)

    def emit(self):
        nc = self.nc
        with ExitStack() as es:
            esem = {e: es.enter_context(nc.semaphore(f"pg_{e}")) for e in self.ENGS}
            dsem = [es.enter_context(nc.semaphore(f"dq_{i}")) for i in range(len(self.pcount))]
            for e in self.ENGS:
                c = 0
                for op in self.ops[e]:
                    if (not op.dma) and op.inc:
                        c += 1
                        op.val = c
            ops, pcount = self.ops, self.pcount

            def make(e):
                def body(eng):
                    waited = {}
                    for op in ops[e]:
                        for d in op.deps:
                            if d.dma:
                                name, s = ("d", d.sem), dsem[d.sem]
                            else:
                                name, s = ("e", d.eng), esem[d.eng]
                            if waited.get(name, 0) < d.val:
                                eng.wait_ge(s, d.val)
                                waited[name] = d.val
                        ins = op.fn(eng)
                        if op.dma:
                            ins.then_inc(dsem[op.sem], op.incv)
                        elif op.inc:
                            ins.then_inc(esem[e], 1)
                    if e == "sp":
                        for k, c in enumerate(pcount):
                            if waited.get(("d", k), 0) < c:
                                eng.wait_ge(dsem[k], c)
                return body

            with nc.Block() as block:
                block.tensor(make("pe"))
                block.scalar(make("act"))
                block.vector(make("dve"))
                block.gpsimd(make("pool"))
                block.sync(make("sp"))


class Ctx:
    def __init__(self, sbuf_kb=200):
        self.nc = bass.Bass("TRN2", target_bir_lowering=False)
        self.es = ExitStack()
        self.p = Prog(self.nc)
        self.big = self.es.enter_context(self.nc.sbuf_tensor("big", [128, sbuf_kb * 512], BF16))
        self.cap = sbuf_kb * 1024
        self.off = 0
        self.psum = self.es.enter_context(self.nc.psum_tensor("psall", [128, 4096], F32))
        self.uid = 0

    def din(self, name, shape, dt):
        return self.nc.dram_tensor(name, list(shape), dt, kind="ExternalInput").ap()

    def dout(self, name, shape, dt):
        return self.nc.dram_tensor(name, list(shape), dt, kind="ExternalOutput").ap()

    def dint(self, name, shape, dt):
        return self.nc.dram_tensor(name, list(shape), dt).ap()

    def t(self, n, dt):
        nb = n * (4 if dt == F32 else 2)
        nb = (nb + 63) // 64 * 64
        assert self.off + nb <= self.cap, f"SBUF overflow {self.off + nb}"
        a = self.off // 2
        v = self.big[:, a:a + n * (2 if dt == F32 else 1)]
        self.off += nb
        return v.bitcast(F32) if dt == F32 else v

    def bank(self, b, nb=1):
        return self.psum[:, b * 512:(b + nb) * 512]

    def key(self, base):
        self.uid += 1
        return (base, self.uid)

    def finish(self):
        self.p.emit()
        self.es.close()
        return self.nc


def v3(ap, a):
    return ap.rearrange("p (a b) -> p a b", a=a)


def emit_p0(c, w_in, g_a, w_out, w_gu, g_f, w_down, w_in_t, w_out_t, w_gu_t, w_down_t, FI, even):
    p = c.p
    ga = c.t(8, F32)
    gf = c.t(8, F32)
    p.dma(ga, g_a, writes=["ga"])
    p.dma(gf, g_f, writes=["gf"])
    NB = 3
    stage = [c.t(5632, F32) for _ in range(NB)]
    bfb = [c.t(5632, BF16) for _ in range(NB)]
    jobs = []
    for k in range(8):
        jobs.append((w_in[k * 128:(k + 1) * 128, :], FI, ga[:, k:k + 1], "gk",
                     [(w_in_t[:, k * FI:(k + 1) * FI], 0, FI)]))
    for k in range(8):
        if even:
            srcs = [(w_out[64 * k:64 * k + 64, :], 0), (w_out[512 + 64 * k:576 + 64 * k, :], 64)]
        else:
            srcs = w_out[k * 128:(k + 1) * 128, :]
        jobs.append((srcs, 1024, None, None, [(w_out_t[:, k * 1024:(k + 1) * 1024], 0, 1024)]))
    gu_v = w_gu_t.rearrange("s p x -> p s x")
    for k in range(8):
        jobs.append((w_gu[k * 128:(k + 1) * 128, :], 5632, gf[:, k:k + 1], "gk",
                     [(gu_v[:, :, k * 512:k * 512 + 256], 0, 2816),
                      (gu_v[:, :, k * 512 + 256:k * 512 + 512], 2816, 2816)]))
    for f in range(22):
        jobs.append((w_down[f * 128:(f + 1) * 128, :], 1024, None, None,
                     [(w_down_t[:, f * 1024:(f + 1) * 1024], 0, 1024)]))
    engs = ["dve", "pool", "act"]
    n = len(jobs)

    def load(j):
        src, ncol, _, _, _ = jobs[j]
        b = j % NB
        if isinstance(src, list):
            for (sap, p0) in src:
                p.dma(stage[b][p0:p0 + 64, 0:ncol], sap, writes=[("st", b)], q="sp", semkey=("st", b))
        else:
            p.dma(stage[b][:, 0:ncol], src, writes=[("st", b)], q="sp")

    load(0)
    load(1)
    for j in range(n):
        src, ncol, gain, _, dsts = jobs[j]
        b = j % NB
        if j + 2 < n:
            load(j + 2)
        eng = engs[j % 3]
        rd = [("st", b), "ga", "gf"]
        if gain is None:
            p.cp(eng, bfb[b][:, 0:ncol], stage[b][:, 0:ncol], reads=rd, writes=[("bf", b)])
        elif eng == "act":
            p.actf(bfb[b][:, 0:ncol], stage[b][:, 0:ncol], AF.Copy, reads=rd, writes=[("bf", b)], scale=gain)
        else:
            p.tsmul(eng, bfb[b][:, 0:ncol], stage[b][:, 0:ncol], gain, reads=rd, writes=[("bf", b)])
        for di, (dst, c0, cn) in enumerate(dsts):
            srcv = bfb[b][:, c0:c0 + cn]
            if len(dst.shape) == 3:
                srcv = srcv.rearrange("p (s x) -> p s x", x=dst.shape[2])
            p.dma(dst, srcv, reads=[("bf", b)], q="sp" if di == 0 else "pool", semkey=("bfst", b, di))


def emit_norm_T(c, src, srckey, t, st, hnb, junk, pst, pstkey, ident, dstT, dstkey, tcol):
    p = c.p
    ss, tm, sd, rs = st[:, 4 * t:4 * t + 1], st[:, 4 * t + 1:4 * t + 2], st[:, 4 * t + 2:4 * t + 3], st[:, 4 * t + 3:4 * t + 4]
    kst = ("st", t)
    p.actf(junk, src, AF.Square, reads=[srckey], writes=["junk", kst], accum_out=ss)
    p.ts("dve", tm, ss, 1.0 / D, EPS, ALU.mult, ALU.add, reads=[kst], writes=[kst])
    p.actf(sd, tm, AF.Sqrt, reads=[kst], writes=[kst])
    p.recip(rs, sd, reads=[kst], writes=[kst])
    p.tsmul("dve", hnb, src, rs, reads=[srckey, kst], writes=["hnb"])
    pb = pst.bitcast(BF16)
    for k in range(8):
        p.tr(pb[:, k * 128:(k + 1) * 128], hnb[:, k * 128:(k + 1) * 128], ident, reads=["hnb", "ident"], writes=[pstkey])
    p.cp("act", dstT[:, :, tcol:tcol + 128], v3(pb[:, 0:1024], 8), reads=[pstkey], writes=[dstkey])
    return rs


def emit_p1(c, even, h, w_in_t, ident, out, g_q=None, g_kv=None, ropeC=None, ropeS=None):
    p = c.p
    FI = FI_E if even else FI_O
    wt = c.t(8 * FI, BF16)
    wt3 = v3(wt, 8)
    for k in range(8):
        p.dma(wt[:, k * FI:(k + 1) * FI], w_in_t[:, k * FI:(k + 1) * FI], writes=[("wt", k)], q=("sp", "pool")[k % 2])
    wkeys = [("wt", k) for k in range(8)]
    hnT = v3(c.t(8 * TPC, BF16), 8)
    st = c.t(64, F32)
    junk = c.t(1024, BF16)
    hb = [c.t(1024, F32) for _ in range(2)]
    hnb = c.t(1024, BF16)
    ntile = TPC // 128
    for t in range(ntile):
        b = t % 2
        p.dma(hb[b], h[t * 128:(t + 1) * 128, :], writes=[("hb", b)], q="sp")
        emit_norm_T(c, hb[b], ("hb", b), t, st, hnb, junk, c.bank(b), ("pst", b), ident, hnT, ("hnT", t // 4), t * 128)
    if even:
        ones = c.t(128, BF16)
        p.memset("pool", ones, 1.0, writes=["ones"])
        gq = c.t(2, F32)
        gkv = c.t(1, F32)
        p.dma(gq, g_q, writes=["gq"])
        p.dma(gkv, g_kv, writes=["gkv"])
        rC = c.t(TPC, F32)
        rS = c.t(TPC, F32)
        p.dma(rC[0:32, :], ropeC, writes=["rC"], q="pool")
        p.dma(rS[0:32, :], ropeS, writes=["rS"], q="pool")
        chunks = [(0, 128, "cq0", 0), (128, 128, "cq1", 128), (256, 128, "ckv", 256),
                  (384, 32, "kr", 384), (416, 32, "krp", None)]
        for j in range(12):
            chunks.append((448 + j * 128, 128, "plain", 512 + j * 128))
    else:
        chunks = [(j * 128, 128, "plain", j * 128) for j in range(10)]
    ob = [c.t(512, BF16) for _ in range(4)]
    sq = [c.t(512, BF16) for _ in range(2)]
    rsb = c.t(512, F32)
    sdb = c.t(512, F32)
    t1 = c.t(512, F32)
    t2 = c.t(512, F32)
    oi = 0
    bi = 0
    evac = ["act", "dve"]
    for g in range(TPC // 512):
        cols = slice(g * 512, (g + 1) * 512)
        held = {}
        for (c0, M, kind, orow) in chunks:
            bnk = 2 + (bi % 6)
            bi += 1
            ps = c.bank(bnk)
            pk = ("ps", bnk)
            for k in range(8):
                p.mm(ps[0:M, :], wt3[:, k, c0:c0 + M], hnT[:, k, cols], start=(k == 0), stop=(k == 7),
                     reads=[("wt", k), ("hnT", g)], writes=[pk])
            if kind == "plain":
                o = ob[oi % 4]
                ok = ("ob", oi % 4)
                oi += 1
                p.cp(evac[oi % 2], o, ps, reads=[pk], writes=[ok])
                p.dma(out[orow:orow + 128, cols], o, reads=[ok], q="sp", semkey=ok)
                continue
            held[kind] = (ps, pk)
            if kind in ("cq1", "ckv"):
                parts = ["cq0", "cq1"] if kind == "cq1" else ["ckv"]
                nfeat = 128.0 * len(parts)
                bnk2 = 2 + (bi % 6)
                bi += 1
                ps2 = c.bank(bnk2)
                pk2 = ("ps", bnk2)
                for i, nm in enumerate(parts):
                    p.actf(sq[i], held[nm][0], AF.Square, reads=[held[nm][1]], writes=[("sq", i)])
                for i, nm in enumerate(parts):
                    p.mm(ps2, ones, sq[i], start=(i == 0), stop=(i == len(parts) - 1),
                         reads=["ones", ("sq", i)], writes=[pk2])
                p.ts("dve", sdb, ps2, 1.0 / nfeat, EPS, ALU.mult, ALU.add, reads=[pk2], writes=["sdb"])
                p.actf(sdb, sdb, AF.Sqrt, reads=["sdb"], writes=["sdb"])
                p.recip(rsb, sdb, reads=["sdb"], writes=["rsb"])
                for i, nm in enumerate(parts):
                    o = ob[oi % 4]
                    ok = ("ob", oi % 4)
                    oi += 1
                    gain = gq[:, i:i + 1] if kind == "cq1" else gkv[:, 0:1]
                    p.stt("dve", o, held[nm][0], gain, rsb, ALU.mult, ALU.mult,
                          reads=[held[nm][1], "rsb", "gq", "gkv"], writes=[ok])
                    orow2 = {"cq0": 0, "cq1": 128, "ckv": 256}[nm]
                    p.dma(out[orow2:orow2 + 128, cols], o, reads=[ok], q="sp", semkey=ok)
            if kind == "krp":
                o = ob[oi % 4]
                ok = ("ob", oi % 4)
                oi += 1
                p.tt("dve", t1[0:32, :], held["kr"][0][0:32, :], rC[0:32, cols], ALU.mult,
                     reads=[held["kr"][1], "rC"], writes=["t1"])
                p.tt("dve", t2[0:32, :], held["krp"][0][0:32, :], rS[0:32, cols], ALU.mult,
                     reads=[held["krp"][1], "rS"], writes=["t2"])
                p.tt("pool", o[0:32, :], t1[0:32, :], t2[0:32, :], ALU.add, reads=["t1", "t2"], writes=[ok])
                p.dma(out[384:416, cols], o[0:32, :], reads=[ok], q="sp", semkey=ok)


def emit_p3(c, final, G2, idx_d, h, w_out_t, w_gu_t, w_down_t, ident, hout, gfin=None):
    p = c.p
    idx = c.t(8, F32).bitcast(mybir.dt.int32)
    p.dma(idx, idx_d, writes=["idx"])
    oTall = v3(c.t(8 * TPC, BF16), 8)
    for k in range(8):
        p.add("pool", (lambda k: lambda e: e.indirect_dma_start(
            out=oTall[:, k, :], out_offset=None, in_=G2,
            in_offset=bass.IndirectOffsetOnAxis(ap=idx[:, k:k + 1], axis=0)))(k),
            reads=["idx"], writes=[("oTall", k)], dma=True, semkey=("oTall", k))
    wo = c.t(8 * 1024, BF16)
    wd = c.t(22 * 1024, BF16)
    for k in range(4):
        p.dma(wo[:, k * 2048:(k + 1) * 2048], w_out_t[:, k * 2048:(k + 1) * 2048], writes=[("wo", k)], q="sp")
    for k in range(11):
        p.dma(wd[:, k * 2048:(k + 1) * 2048], w_down_t[:, k * 2048:(k + 1) * 2048], writes=[("wd", k)], q="pool")
    wo3 = v3(wo, 8)
    wd3 = v3(wd, 22)
    if final:
        gf = c.t(1024, F32)
        p.dma(gf, gfin, writes=["gfin"])
    hb = [c.t(1024, F32) for _ in range(2)]
    hmid = v3(c.t(4 * 1024, F32), 4)
    hn2T = v3(c.t(8 * 512, BF16), 8)
    actT = v3(c.t(22 * 512, BF16), 22)
    wgu = [c.t(4096, BF16) for _ in range(3)]
    st = c.t(64, F32)
    st2 = c.t(64, F32)
    junk = c.t(1024, BF16)
    hnb = c.t(1024, BF16)
    sg = [c.t(512, F32) for _ in range(2)]
    ho = [c.t(1024, F32) for _ in range(2)]
    NG = TPC // 512
    gi = 0
    sgi = 0
    wl = 0

    def load_wgu(idx):
        b = idx % 3
        sf = idx % 11
        p.dma(wgu[b], w_gu_t[sf], writes=[("wgu", b)], q="sp")

    load_wgu(0)
    load_wgu(1)
    for g in range(NG):
        for tt in range(4):
            t = g * 4 + tt
            b = t % 2
            p.dma(hb[b], h[t * 128:(t + 1) * 128, :], writes=[("hb", b)], q="pool")
            psM = c.bank(2 * b, 2)
            pk = ("psMD", b)
            for half in range(2):
                for k in range(8):
                    p.mm(psM[:, half * 512:(half + 1) * 512], oTall[:, k, t * 128:(t + 1) * 128],
                         wo3[:, k, half * 512:(half + 1) * 512], start=(k == 0), stop=(k == 7),
                         reads=[("oTall", k), ("wo", k // 2)], writes=[pk])
            p.tt("dve", hmid[:, tt, :], psM, hb[b], ALU.add, reads=[pk, ("hb", b)], writes=[("hmid", tt)])
            emit_norm_T(c, hmid[:, tt, :], ("hmid", tt), t % 16, st, hnb, junk, c.bank(4), "pst", ident, hn2T, "hn2T", tt * 128)
        for sf in range(11):
            wb = wl % 3
            w3 = v3(wgu[wb], 8)
            for fi in range(2):
                f = 2 * sf + fi
                bG = 5 + (gi % 3)
                gi += 1
                bU = 5 + (gi % 3)
                gi += 1
                psG, psU = c.bank(bG), c.bank(bU)
                for k in range(8):
                    p.mm(psG, w3[:, k, fi * 128:(fi + 1) * 128], hn2T[:, k, :], start=(k == 0), stop=(k == 7),
                         reads=[("wgu", wb), "hn2T"], writes=[("psGU", bG)])
                for k in range(8):
                    p.mm(psU, w3[:, k, 256 + fi * 128:256 + (fi + 1) * 128], hn2T[:, k, :], start=(k == 0), stop=(k == 7),
                         reads=[("wgu", wb), "hn2T"], writes=[("psGU", bU)])
                s = sg[sgi % 2]
                sk = ("sg", sgi % 2)
                sgi += 1
                p.actf(s, psG, AF.Silu, reads=[("psGU", bG)], writes=[sk])
                p.tt("dve", actT[:, f, :], psU, s, ALU.mult, reads=[("psGU", bU), sk], writes=[("actT", f)])
            wl += 1
            if wl + 1 < NG * 11:
                load_wgu(wl + 1)
        for tt in range(4):
            t = g * 4 + tt
            b = t % 2
            psD = c.bank(2 * b, 2)
            pk = ("psMD", b)
            for half in range(2):
                for f in range(22):
                    p.mm(psD[:, half * 512:(half + 1) * 512], actT[:, f, tt * 128:(tt + 1) * 128],
                         wd3[:, f, half * 512:(half + 1) * 512], start=(f == 0), stop=(f == 21),
                         reads=[("actT", f), ("wd", f // 2)], writes=[pk])
            o = ho[b]
            okey = ("ho", b)
            p.tt("dve", o, psD, hmid[:, tt, :], ALU.add, reads=[pk, ("hmid", tt)], writes=[okey])
            if final:
                tq = t % 16
                ss, tm, sd, rs = st2[:, 4 * tq:4 * tq + 1], st2[:, 4 * tq + 1:4 * tq + 2], st2[:, 4 * tq + 2:4 * tq + 3], st2[:, 4 * tq + 3:4 * tq + 4]
                kst = ("st2", tq)
                p.actf(junk, o, AF.Square, reads=[okey], writes=["junk", kst], accum_out=ss)
                p.ts("dve", tm, ss, 1.0 / D, EPS, ALU.mult, ALU.add, reads=[kst], writes=[kst])
                p.actf(sd, tm, AF.Sqrt, reads=[kst], writes=[kst])
                p.recip(rs, sd, reads=[kst], writes=[kst])
                p.stt("dve", o, o, rs, gf, ALU.mult, ALU.mult, reads=[okey, kst, "gfin"], writes=[okey])
            p.dma(hout[t * 128:(t + 1) * 128, :], o, reads=[okey], q="sp", semkey=okey)


def emit_normalize(c, osb, okey, out_ap, pbank, pkey, onesf, rc, ob, obkey, esink=None):
    p = c.p
    if esink is not None:
        p.tsadd("dve", osb[64:65, :], osb[64:65, :], esink, reads=[okey, "esink"], writes=[okey])
    p.recip(rc[64:65, :], osb[64:65, :], reads=[okey], writes=["rc"])
    p.mm(pbank[0:64, :], onesf[64:65, 0:64], rc[64:65, :], reads=["rc", "onesf"], writes=[pkey])
    p.tt("dve", ob[0:64, :], pbank[0:64, :], osb[0:64, :], ALU.mult, reads=[pkey, okey], writes=[obkey])
    p.dma(out_ap, ob[0:64, :], reads=[obkey], q="sp", semkey=obkey)


def strided_cols(tile, rows, d, r, b):
    if d == 1:
        return tile[rows, 128 * b:128 * (b + 1)]
    v = tile[rows, :].rearrange("p (n r) -> p r n", r=d)
    return v[:, r, 128 * b:128 * (b + 1)]


def emit_band_group(c, blocks, QT, KT, vaug, tabs, ident, scale, sbank, skey, PT, ptkey, po, pokey,
                    qkeys, kkeys, vkeys):
    p = c.p
    for half in range(2):
        f0, f1 = blocks[2 * half][5], blocks[2 * half + 1][5]
        variant = 2 if (f0 and f1) else (1 if f0 else 0)
        ps = sbank[:, half * 512:(half + 1) * 512]
        p.mm(ps, ident, tabs[:, variant, :], start=True, stop=False, reads=["ident", "tabs"], writes=[skey])
        for cc in range(2):
            d, r, b, vp, vc, first = blocks[2 * half + cc]
            q = strided_cols(QT, slice(0, 64), d, r, b)
            kc = strided_cols(KT, slice(0, 64), d, r, b)
            kp = kc if first else strided_cols(KT, slice(0, 64), d, r, b - 1)
            base = cc * 256
            p.mm(ps[:, base:base + 128], kp, q, start=False, stop=False, reads=qkeys + kkeys, writes=[skey])
            p.mm(ps[:, base + 128:base + 256], kc, q, start=False, stop=(cc == 1), reads=qkeys + kkeys, writes=[skey])
    p.actf(PT, sbank, AF.Exp, reads=[skey], writes=[ptkey], scale=scale)
    for cc in range(4):
        d, r, b, vp, vc, first = blocks[cc]
        if first:
            vp = vc
        o = po[0:65, cc * 128:(cc + 1) * 128]
        p.mm(o, vaug[:, vp, 0:65], PT[:, cc * 256:cc * 256 + 128], start=True, stop=False,
             reads=[ptkey] + vkeys, writes=[pokey])
        p.mm(o, vaug[:, vc, 0:65], PT[:, cc * 256 + 128:cc * 256 + 256], start=False, stop=True,
             reads=[ptkey] + vkeys, writes=[pokey])


def emit_build_vaug(c, vT, vkey, vaug, vaugkey, ident, blocks, pbank, pkey):
    p = c.p
    pb = pbank.bitcast(BF16)
    for i0 in range(0, len(blocks), 8):
        grp = blocks[i0:i0 + 8]
        for j, (idx, d, r, b) in enumerate(grp):
            p.tr(pb[:, j * 64:(j + 1) * 64], strided_cols(vT, slice(0, 64), d, r, b), ident[0:64, 0:64],
                 reads=(list(vkey) if isinstance(vkey, list) else [vkey]) + ["ident"], writes=[pkey])
        idx0 = grp[0][0]
        n = len(grp)
        p.cp("dve", vaug[:, idx0:idx0 + n, 0:64], pb[:, 0:n * 64].rearrange("p (a b) -> p a b", a=n),
             reads=[pkey], writes=[vaugkey])


def emit_p2_even(c, G1, idx_d, w_uq_x, w_ukv, ropeC, ropeS, cmask_d, dtab_d, ident, onesf, out_ap):
    p = c.p
    NCH = S // 128
    FG = 2048
    base_off = c.off
    wuq_f = c.t(2 * 192, F32)
    wuq = c.t(2 * 192, BF16)
    wukv_f = c.t(128, F32)
    wukv = c.t(128, BF16)
    p.dma(v3(wuq_f, 2), w_uq_x.rearrange("(k p) n -> p k n", p=128), writes=["wuq_f"])
    p.dma(wukv_f, w_ukv, writes=["wukv_f"])
    p.cp("dve", wuq, wuq_f, reads=["wuq_f"], writes=["wuq"])
    p.cp("dve", wukv, wukv_f, reads=["wukv_f"], writes=["wukv"])
    wuq3 = v3(wuq, 2)
    cmask = c.t(4 * 512, BF16)
    p.dma(cmask, cmask_d, writes=["cmask"])
    cmask3 = v3(cmask, 4)
    QT = c.t(S, BF16)
    KT = c.t(S, BF16)
    vaug = c.t(NCH * 66, BF16).rearrange("p (a b) -> p a b", b=66)
    p.memset("pool", vaug, 1.0, writes=["vaug"])
    cq = [v3(c.t(2 * 2048, BF16), 2) for _ in range(2)]
    ckv = [c.t(2048, BF16) for _ in range(2)]
    rC = [c.t(2048, F32) for _ in range(2)]
    rS = [c.t(2048, F32) for _ in range(2)]
    t1 = c.t(512, F32)
    t2 = c.t(512, F32)
    ring = [0]

    def nextbank():
        b = ring[0] % 6
        ring[0] += 1
        return b

    for blk in range(S // 2048):
        bb = blk % 2
        csl = slice(blk * 2048, (blk + 1) * 2048)
        p.dma(cq[bb], G1[blk * FG:blk * FG + 256, :].rearrange("(k p) t -> p k t", p=128), writes=[("cq", bb)], q="sp")
        p.dma(ckv[bb], G1[blk * FG + 256:blk * FG + 384, :], writes=[("ckv", bb)], q="pool")
        p.dma(KT[64:96, csl], G1[blk * FG + 384:blk * FG + 416, :], writes=[("KTr", blk)], q="pool")
        p.dma(rC[bb][64:96, :], ropeC[:, csl], writes=[("rC", bb)], q="sp")
        p.dma(rS[bb][64:96, :], ropeS[:, csl], writes=[("rS", bb)], q="pool")
        for sub in range(4):
            si = blk * 4 + sub
            ssl = slice(sub * 512, (sub + 1) * 512)
            gsl = slice(si * 512, (si + 1) * 512)
            bA, bB, bK = nextbank(), nextbank(), nextbank()
            psA, psB, psK = c.bank(bA), c.bank(bB), c.bank(bK)
            for k in range(2):
                p.mm(psA[0:96, :], wuq3[:, k, 0:96], cq[bb][:, k, ssl], start=(k == 0), stop=(k == 1),
                     reads=["wuq", ("cq", bb)], writes=[("ps", bA)])
            for k in range(2):
                p.mm(psB[0:96, :], wuq3[:, k, 96:192], cq[bb][:, k, ssl], start=(k == 0), stop=(k == 1),
                     reads=["wuq", ("cq", bb)], writes=[("ps", bB)])
            p.mm(psK[0:64, :], wukv[:, 0:64], ckv[bb][:, ssl], reads=["wukv", ("ckv", bb)], writes=[("ps", bK)])
            p.cp("act", QT[0:64, gsl], psA[0:64, :], reads=[("ps", bA)], writes=[("QTn", si)])
            p.tt("dve", t1[64:96, :], psA[64:96, :], rC[bb][64:96, ssl], ALU.mult, reads=[("ps", bA), ("rC", bb)], writes=["t1"])
            p.tt("dve", t2[64:96, :], psB[64:96, :], rS[bb][64:96, ssl], ALU.mult, reads=[("ps", bB), ("rS", bb)], writes=["t2"])
            p.tt("pool", QT[64:96, gsl], t1[64:96, :], t2[64:96, :], ALU.add, reads=["t1", "t2"], writes=[("QTr", si)])
            p.cp("act", KT[0:64, gsl], psK[0:64, :], reads=[("ps", bK)], writes=[("KTn", si)])
            if sub % 2 == 0:
                bV = nextbank()
                psV = c.bank(bV)
            for j in range(4):
                jj = (sub % 2) * 4 + j
                p.mm(psV[:, jj * 64:(jj + 1) * 64], ckv[bb][:, sub * 512 + j * 128:sub * 512 + (j + 1) * 128], wukv[:, 64:128],
                     reads=["wukv", ("ckv", bb)], writes=[("ps", bV)])
            if sub % 2 == 1:
                c0 = si * 4 - 4
                p.cp("dve", vaug[:, c0:c0 + 8, 0:64], psV.rearrange("p (a b) -> p a b", a=8),
                     reads=[("ps", bV)], writes=["vaug"])
    p.barrier()
    scale = 1.0 / math.sqrt(96.0)
    PT = [c.t(1024, BF16) for _ in range(3)]
    osb = [c.t(512, F32) for _ in range(2)]
    rc = c.t(512, F32)
    ob = [c.t(512, BF16) for _ in range(2)]
    pend = []
    gp = [0]

    def flush_pv():
        (j, i0, nk, sbi, pbi) = pend.pop(0)
        for e in range(2):
            i = i0 + e
            p.mm(c.bank(6 + j % 2)[0:65, :], vaug[:, i, 0:65], PT[pbi][:, e * 512:(e + 1) * 512],
                 start=(i == 0), stop=(i == nk - 1), reads=[("PT", pbi)], writes=[("po", j % 2)])
        if i0 + 2 == nk:
            jb = j % 2
            p.cp("dve", osb[jb][0:65, :], c.bank(6 + jb)[0:65, :], reads=[("po", jb)], writes=[("osb", jb)])
            sbn = gp[0] % 3
            gp[0] += 1
            emit_normalize(c, osb[jb], ("osb", jb), out_ap(0, j * 512), c.bank(2 * sbn), ("sT", sbn),
                           onesf, rc, ob[jb], ("ob", jb))

    for j in range(S // 512):
        nk = 4 * (j + 1)
        for pr in range(nk // 2):
            sbi = gp[0] % 3
            pbi = gp[0] % 3
            gp[0] += 1
            sT = c.bank(2 * sbi, 2)
            for e in range(2):
                i = 2 * pr + e
                ps = sT[:, e * 512:(e + 1) * 512]
                kq = (KT[0:96, i * 128:(i + 1) * 128], QT[0:96, j * 512:(j + 1) * 512])
                if i >= 4 * j:
                    p.mm(ps, ident, cmask3[:, i - 4 * j, :], start=True, stop=False, reads=["ident", "cmask"], writes=[("sT", sbi)])
                    p.mm(ps, kq[0], kq[1], start=False, stop=True, writes=[("sT", sbi)])
                else:
                    p.mm(ps, kq[0], kq[1], start=True, stop=True, writes=[("sT", sbi)])
            p.actf(PT[pbi], sT, AF.Exp, reads=[("sT", sbi)], writes=[("PT", pbi)], scale=scale)
            pend.append((j, 2 * pr, nk, sbi, pbi))
            if len(pend) > 1:
                flush_pv()
    while pend:
        flush_pv()
    p.barrier()
    c.off = base_off
    qT = c.t(S, BF16)
    kT = c.t(S, BF16)
    vT = c.t(S, BF16)
    idx = c.t(24, F32).bitcast(mybir.dt.int32)
    p.dma(idx, idx_d, writes=["idx"])
    prevk = []
    for ti, (tl, nm) in enumerate(((qT, "qT"), (kT, "kT"), (vT, "vT"))):
        for r in range(8):
            p.add("pool", (lambda tl, r, j: lambda e: e.indirect_dma_start(
                out=tl[0:64, r * 2048:(r + 1) * 2048], out_offset=None, in_=G1,
                in_offset=bass.IndirectOffsetOnAxis(ap=idx[0:64, j:j + 1], axis=0)))(tl, r, ti * 8 + r),
                reads=["idx"] + prevk, writes=[(nm, r)], dma=True, semkey=(nm, r))
            prevk = [(nm, r)] if r % 4 == 3 else prevk
    qkeys = [("qT", r) for r in range(8)]
    kkeys = [("kT", r) for r in range(8)]
    vkeys8 = [("vT", r) for r in range(8)]
    tabs = c.t(9 * 512, BF16)
    p.dma(tabs, dtab_d, writes=["tabs"], q="pool")
    tabs4 = tabs.rearrange("p (a b x) -> p a b x", a=3, b=3)
    pats = (1, 4, 16)
    vaugs = []
    for pi, d in enumerate(pats):
        va = c.t(NCH * 66, BF16).rearrange("p (a b) -> p a b", b=66)
        p.memset("pool", va, 1.0, writes=[("vaug", pi)])
        nper = NCH // d
        blocks = [(r * nper + b, d, r, b) for r in range(d) for b in range(nper)]
        emit_build_vaug(c, vT, vkeys8, va, ("vaug", pi), ident, blocks, c.bank(7), ("ps", 7))
        vaugs.append(va)
    p.barrier()
    PTd = [c.t(1024, BF16) for _ in range(2)]
    oacc = c.t(2048, F32)
    rc = c.t(512, F32)
    ob = [c.t(512, BF16) for _ in range(2)]
    dscale = 0.125
    gi = 0
    ni = 0
    for seg in range(S // 2048):
        for pi, d in enumerate(pats):
            nper = NCH // d
            bps = 16 // d
            blist = []
            for r in range(d):
                for bl in range(bps):
                    b = seg * bps + bl
                    blist.append((d, r, b, r * nper + b - 1, r * nper + b, b == 0))
            for g in range(4):
                sbi = gi % 3
                pti = gi % 2
                pob = 6 + gi % 2
                gi += 1
                emit_band_group(c, blist[4 * g:4 * g + 4], qT, kT, vaugs[pi], tabs4[:, pi, :, :], ident, dscale,
                                c.bank(2 * sbi, 2), ("sT", sbi), PTd[pti], ("PT", pti), c.bank(pob), ("po", pob),
                                [], [], [])
                po = c.bank(pob)[0:65, :]
                if d == 1:
                    dst = oacc[0:65, 512 * g:512 * (g + 1)]
                    src = po
                elif d == 4:
                    dst = oacc[0:65, :].rearrange("p (i r) -> p r i", r=4)[:, g, :]
                    src = po
                else:
                    dst = oacc[0:65, :].rearrange("p (i r) -> p r i", r=16)[:, 4 * g:4 * g + 4, :]
                    src = po.rearrange("p (a b) -> p a b", a=4)
                if pi == 0:
                    p.cp("dve", dst, src, reads=[("po", pob)], writes=["oacc"])
                else:
                    p.tt("dve", dst, src, dst, ALU.add, reads=[("po", pob), "oacc"], writes=["oacc"])
        for ch in range(4):
            sbn = gi % 3
            gi += 1
            emit_normalize(c, oacc[:, ch * 512:(ch + 1) * 512], "oacc",
                           out_ap(1, seg * 2048 + ch * 512),
                           c.bank(2 * sbn), ("sT", sbn), onesf, rc, ob[ni % 2], ("ob", ni % 2))
            ni += 1


def emit_p2_odd(c, G1, idx_d, sinks_d, stab_d, ident, onesf, out_ap):
    p = c.p
    NCH = S // 128
    esink = c.t(2, F32)
    p.dma(esink[64:65, :], sinks_d, writes=["esink"])
    p.actf(esink[64:65, :], esink[64:65, :], AF.Exp, reads=["esink"], writes=["esink"])
    qTs = [c.t(S, BF16) for _ in range(2)]
    kT = c.t(S, BF16)
    vT = c.t(S, BF16)
    idx = c.t(32, F32).bitcast(mybir.dt.int32)
    p.dma(idx, idx_d, writes=["idx"])
    prevk = []
    for ti, (tl, nm) in enumerate(((qTs[0], "q0"), (qTs[1], "q1"), (kT, "kT"), (vT, "vT"))):
        for r in range(8):
            p.add("pool", (lambda tl, r, j: lambda e: e.indirect_dma_start(
                out=tl[0:64, r * 2048:(r + 1) * 2048], out_offset=None, in_=G1,
                in_offset=bass.IndirectOffsetOnAxis(ap=idx[0:64, j:j + 1], axis=0)))(tl, r, ti * 8 + r),
                reads=["idx"] + prevk, writes=[(nm, r)], dma=True, semkey=(nm, r))
            prevk = [(nm, r)] if r % 4 == 3 else prevk
    vkeys8 = [("vT", r) for r in range(8)]
    tabs = c.t(2 * 3 * 512, BF16)
    p.dma(tabs, stab_d, writes=["tabs"], q="pool")
    tabs4 = tabs.rearrange("p (a b x) -> p a b x", a=2, b=3)
    vaug = c.t(NCH * 66, BF16).rearrange("p (a b) -> p a b", b=66)
    p.memset("pool", vaug, 1.0, writes=["vaug"])
    emit_build_vaug(c, vT, vkeys8, vaug, "vaug", ident, [(b, 1, 0, b) for b in range(NCH)], c.bank(7), ("ps", 7))
    p.barrier()
    PT = [c.t(1024, BF16) for _ in range(2)]
    osb = [c.t(512, F32) for _ in range(2)]
    rc = c.t(512, F32)
    ob = [c.t(512, BF16) for _ in range(2)]
    scale = 0.125
    gi = 0
    for hh in range(2):
        for g in range(NCH // 4):
            blist = [(1, 0, b, b - 1, b, b == 0) for b in range(4 * g, 4 * g + 4)]
            sbi = gi % 3
            pti = gi % 2
            pob = 6 + gi % 2
            gi += 1
            emit_band_group(c, blist, qTs[hh], kT, vaug, tabs4[:, hh, :, :], ident, scale,
                            c.bank(2 * sbi, 2), ("sT", sbi), PT[pti], ("PT", pti), c.bank(pob), ("po", pob),
                            [], [], [])
            jb = gi % 2
            p.cp("act", osb[jb][0:65, :], c.bank(pob)[0:65, :], reads=[("po", pob)], writes=[("osb", jb)])
            sbn = gi % 3
            gi += 1
            emit_normalize(c, osb[jb], ("osb", jb), out_ap(hh, g * 512),
                           c.bank(2 * sbn), ("sT", sbn), onesf, rc, ob[jb], ("ob", jb), esink=esink[64:65, hh:hh + 1])


I32 = mybir.dt.int32


def build_fused(nlayers=4):
    c = Ctx()
    p = c.p
    nc = c.nc
    x = c.din("x", [TPC, D], F32)
    out = c.dout("out", [TPC, D], F32)
    ident_d = c.din("ident", [128, 128], BF16)
    ropeC = c.din("ropeC", [32, S], F32)
    ropeS = c.din("ropeS", [32, S], F32)
    ropeCo = c.din("ropeC_own", [32, TPC], F32)
    ropeSo = c.din("ropeS_own", [32, TPC], F32)
    cmask = c.din("cmask", [128, 4 * 512], BF16)
    dtab = c.din("dtab", [128, 9 * 512], BF16)
    stab = c.din("stab", [128, 6 * 512], BF16)
    idx_e = c.din("idx_e", [128, 24], I32)
    idx_o = c.din("idx_o", [128, 32], I32)
    idx_p3 = c.din("idx_p3", [128, 8], I32)
    gfin = c.din("gfin", [128, D], F32)
    L = []
    for l in range(nlayers):
        even = l % 2 == 0
        FI = FI_E if even else FI_O
        d = {"w_in": c.din(f"w_in{l}", [D, FI], F32), "g_a": c.din(f"g_a{l}", [128, 8], F32),
             "w_out": c.din(f"w_out{l}", [D, D], F32), "w_gu": c.din(f"w_gu{l}", [D, 2 * DFF], F32),
             "g_f": c.din(f"g_f{l}", [128, 8], F32), "w_down": c.din(f"w_down{l}", [DFF, D], F32)}
        if even:
            d["g_q"] = c.din(f"g_q{l}", [128, 2], F32)
            d["g_kv"] = c.din(f"g_kv{l}", [128, 1], F32)
            d["w_uq_x"] = c.din(f"w_uq_x{l}", [256, 192], F32)
            d["w_ukv"] = c.din(f"w_ukv{l}", [128, 128], F32)
        else:
            d["sinks"] = c.din(f"sinks{l}", [1, 2], F32)
        L.append(d)
    w_in_t = {True: c.dint("w_in_t_e", [128, 8 * FI_E], BF16), False: c.dint("w_in_t_o", [128, 8 * FI_O], BF16)}
    w_out_t = c.dint("w_out_t", [128, 8 * 1024], BF16)
    w_gu_t = c.dint("w_gu_t", [11, 128, 8 * 512], BF16)
    w_down_t = c.dint("w_down_t", [128, 22 * 1024], BF16)
    projT = {True: c.dint("projT_e", [2048, TPC], BF16), False: c.dint("projT_o", [FI_O, TPC], BF16)}
    G1 = {True: c.dint("G1_e", [8 * 2048, TPC], BF16), False: c.dint("G1_o", [8 * FI_O, TPC], BF16)}
    oTs = c.dint("oTs", [8 * 128, TPC], BF16)
    G2 = c.dint("G2", [8 * 8 * 128, TPC], BF16)
    h_i = c.dint("h_i", [TPC, D], F32)
    groups = [list(range(NCORES))]

    def out_ap(kind, tok0):
        r0 = (tok0 // TPC) * 128 + kind * 64
        c0 = tok0 % TPC
        return oTs[r0:r0 + 64, c0:c0 + 512]

    ident = c.t(128, BF16)
    onesf = c.t(64, F32)
    p.dma(ident, ident_d, writes=["ident"])
    p.memset("pool", onesf, 1.0, writes=["onesf"])
    base = c.off
    for l in range(nlayers):
        even = l % 2 == 0
        FI = FI_E if even else FI_O
        d = L[l]
        hin = x if l == 0 else h_i
        hout = out if l == nlayers - 1 else h_i
        p.barrier()
        c.off = base
        emit_p0(c, d["w_in"], d["g_a"], d["w_out"], d["w_gu"], d["g_f"], d["w_down"],
                w_in_t[even], w_out_t, w_gu_t, w_down_t, FI, even)
        p.barrier()
        c.off = base
        if even:
            emit_p1(c, True, hin, w_in_t[True], ident, projT[True], d["g_q"], d["g_kv"], ropeCo, ropeSo)
        else:
            emit_p1(c, False, hin, w_in_t[False], ident, projT[False])
        p.barrier()
        p.add("pool", (lambda a, b: lambda e: e.collective_compute(
            "AllGather", ALU.bypass, replica_groups=groups, ins=[a], outs=[b]))(projT[even], G1[even]),
            dma=True, semkey="cc", incv=1)
        p.barrier()
        c.off = base
        if even:
            emit_p2_even(c, G1[True], idx_e, d["w_uq_x"], d["w_ukv"], ropeC, ropeS, cmask, dtab, ident, onesf, out_ap)
        else:
            emit_p2_odd(c, G1[False], idx_o, d["sinks"], stab, ident, onesf, out_ap)
        p.barrier()
        p.add("pool", lambda e: e.collective_compute(
            "AllGather", ALU.bypass, replica_groups=groups, ins=[oTs], outs=[G2]),
            dma=True, semkey="cc", incv=1)
        p.barrier()
        c.off = base
        emit_p3(c, l == 3, G2, idx_p3, hin, w_out_t, w_gu_t, w_down_t, ident, hout, gfin if l == 3 else None)
    return c.finish()


_PROGS = {}
_CONST = {}


def _band_tables(slope, dil, incl, qscale):
    k = np.arange(128)[:, None]
    q = np.arange(128)[None, :]
    dp = (128 + q - k).astype(np.float32)
    dc = (q - k).astype(np.float32)
    vp = (dp <= 128) if incl else (dp < 128)
    vc = dc >= 0
    prevb = np.where(vp, -slope * dil * dp / qscale, NEG).astype(np.float32)
    curb = np.where(vc, -slope * dil * dc / qscale, NEG).astype(np.float32)
    mask = np.full((128, 128), NEG, np.float32)
    t0 = np.concatenate([prevb, curb, prevb, curb], 1)
    t1 = np.concatenate([mask, curb, prevb, curb], 1)
    t2 = np.concatenate([mask, curb, mask, curb], 1)
    return np.stack([t0, t1, t2], 1)


def _consts():
    if _CONST:
        return _CONST
    pos = np.arange(S, dtype=np.float32)
    inv = (np.float32(10000.0) ** (-np.arange(0, 32, 2, dtype=np.float32) / np.float32(32))).astype(np.float32)
    ang = (pos[:, None] * inv[None, :]).astype(np.float32)
    cos = np.cos(ang).astype(np.float32).T
    sin = np.sin(ang).astype(np.float32).T
    _CONST["ropeC"] = np.ascontiguousarray(np.concatenate([cos, cos], 0))
    _CONST["ropeS"] = np.ascontiguousarray(np.concatenate([-sin, sin], 0))
    _CONST["ident"] = np.eye(128, dtype=np.float32).astype(NPBF)
    kk = np.arange(128)[:, None]
    qq = np.arange(512)[None, :]
    cm = np.stack([np.where(128 * d + kk <= qq, 0.0, NEG) for d in range(4)], 1).astype(np.float32)
    _CONST["cmask"] = np.ascontiguousarray(cm.reshape(128, 2048).astype(NPBF))
    dt = []
    for hd in range(8):
        slope = 2.0 ** (-8.0 * (hd + 1) / 8)
        t = np.stack([_band_tables(slope, d, True, 0.125) for d in (1, 4, 16)], 1)
        dt.append(np.ascontiguousarray(t.reshape(128, 9 * 512).astype(NPBF)))
    _CONST["dtab"] = dt
    stt = []
    for cc in range(8):
        t = np.stack([_band_tables(2.0 ** (-8.0 * (2 * cc + hh + 1) / 16), 1, False, 0.125) for hh in range(2)], 1)
        stt.append(np.ascontiguousarray(t.reshape(128, 6 * 512).astype(NPBF)))
    _CONST["stab"] = stt
    pp = np.arange(128, dtype=np.int64)[:, None]
    rr = np.arange(8, dtype=np.int64)[None, :]
    ie, io, i3 = [], [], []
    for cc in range(8):
        e = np.concatenate([rr * 2048 + b0 + 64 * cc + pp for b0 in (512, 1024, 1536)], 1)
        o = np.concatenate([rr * FI_O + 128 * cc + pp, rr * FI_O + 128 * cc + 64 + pp,
                            rr * FI_O + 1024 + 64 * (cc // 4) + pp, rr * FI_O + 1152 + 64 * (cc // 4) + pp], 1)
        ie.append(np.ascontiguousarray(e.astype(np.int32)))
        io.append(np.ascontiguousarray(o.astype(np.int32)))
        i3.append(np.ascontiguousarray((rr * 1024 + cc * 128 + pp).astype(np.int32)))
    _CONST["idx_e"], _CONST["idx_o"], _CONST["idx_p3"] = ie, io, i3
    return _CONST


def _g2(g):
    return np.ascontiguousarray(g.reshape(-1, 128).T.astype(np.float32))


def make_maps(inp, nlayers=4):
    C = _consts()
    ca = np.ascontiguousarray
    x = inp["x"].reshape(S, D).astype(np.float32)
    shared = {"ident": C["ident"], "ropeC": C["ropeC"], "ropeS": C["ropeS"], "cmask": C["cmask"],
              "gfin": ca(np.broadcast_to(inp["final_norm"].astype(np.float32)[None, :], (128, D)))}
    for l in range(nlayers):
        i = l // 2
        if l % 2 == 0:
            w = inp["e_w_in"][i]
            shared[f"w_in{l}"] = ca(np.concatenate([w[:, 0:416], w[:, 400:416], w[:, 384:400], w[:, 416:1952]], 1))
            shared[f"w_out{l}"] = ca(inp["e_w_out"][i])
            shared[f"g_q{l}"] = _g2(inp["e_q_norm"][i])
            shared[f"g_kv{l}"] = _g2(inp["e_kv_norm"][i])
        else:
            shared[f"w_in{l}"] = ca(inp["o_w_qkv"][i])
            shared[f"w_out{l}"] = ca(inp["o_w_out"][i])
        shared[f"g_a{l}"] = _g2(inp["attn_norm"][l])
        shared[f"g_f{l}"] = _g2(inp["ffn_norm"][l])
        shared[f"w_gu{l}"] = ca(inp["f_w_gate_up"][l])
        shared[f"w_down{l}"] = ca(inp["f_w_down"][l])
    maps = []
    for c in range(NCORES):
        m = dict(shared)
        m["x"] = ca(x[c * TPC:(c + 1) * TPC])
        m["ropeC_own"] = ca(C["ropeC"][:, c * TPC:(c + 1) * TPC])
        m["ropeS_own"] = ca(C["ropeS"][:, c * TPC:(c + 1) * TPC])
        m["dtab"] = C["dtab"][c]
        m["stab"] = C["stab"][c]
        m["idx_e"], m["idx_o"], m["idx_p3"] = C["idx_e"][c], C["idx_o"][c], C["idx_p3"][c]
        for l in range(nlayers):
            i = l // 2
            if l % 2 == 0:
                wq = inp["e_w_uq"][i][:, c * 96:(c + 1) * 96]
                m[f"w_uq_x{l}"] = ca(np.concatenate([wq, np.zeros((256, 64), np.float32), wq[:, 80:96], wq[:, 64:80]], 1))
                m[f"w_ukv{l}"] = ca(inp["e_w_ukv"][i][:, c * 128:(c + 1) * 128])
            else:
                m[f"sinks{l}"] = ca(inp["o_sinks"][i][2 * c:2 * c + 2].reshape(1, 2).astype(np.float32))
        maps.append(m)
    return maps


def run_fused(inp, nlayers=4):
    if nlayers not in _PROGS:
        _PROGS[nlayers] = build_fused(nlayers)
    res = run_bass_kernel_spmd(_PROGS[nlayers], make_maps(inp, nlayers), core_ids=list(range(NCORES)))
    return np.concatenate([r["out"] for r in res.results], axis=0)


def kernel(**inputs):
    inp = {k: np.asarray(v) for k, v in inputs.items()}
    return run_fused(inp, 4).reshape(1, S, D).astype(np.float32)
```
